# Optimizing a Trainium2 kernel written in Bass

```python
import math
import jax, jax.numpy as jnp
from jax import lax
import numpy as np

D_MODEL = 1024
BATCH = 1
SEQ = 16384
DEPTH = 4

CHUNK = 128
D_MIX = D_MODEL
GMLP_HEADS = 4
GMLP_HEAD_DIM = 64
GMLP_W = GMLP_HEADS * GMLP_HEAD_DIM
SSD_HEADS = 6
SSD_HEAD_DIM = 64
SSD_W = SSD_HEADS * SSD_HEAD_DIM
SSD_GROUPS = 2
SSD_STATE = 64
SSD_CONV = 4
SSD_XBC = SSD_W + 2 * SSD_GROUPS * SSD_STATE
MLA_HEADS = 6
MLA_NOPE = 64
MLA_ROPE = 32
MLA_V = 64
MLA_W = MLA_HEADS * MLA_V
MLA_Q_RANK = 384
MLA_KV_RANK = 256
ROPE_BASE = 10000.0
D_FF = 4 * D_MODEL
EPS = 1e-6
IN_WIDTHS = (2 * GMLP_W, SSD_W, SSD_XBC, SSD_HEADS, MLA_Q_RANK, MLA_KV_RANK, MLA_ROPE)
D_IN = sum(IN_WIDTHS)

kernel_name = "hybrid_gmlp_ssd_mla_block"


def rmsnorm(x, g):
    xf = x.astype(jnp.float32)
    var = jnp.mean(xf * xf, axis=-1, keepdims=True)
    return (xf * lax.rsqrt(var + EPS)).astype(x.dtype) * g


def split_cols(proj):
    idx, acc = [], 0
    for w in IN_WIDTHS[:-1]:
        acc += w
        idx.append(acc)
    return jnp.split(proj, idx, axis=-1)


def rope_tables(S):
    half = MLA_ROPE // 2
    pos = jnp.arange(S, dtype=jnp.float32)
    inv_freq = jnp.power(ROPE_BASE, -jnp.arange(half, dtype=jnp.float32) / half)
    ang = pos[:, None] * inv_freq[None, :]
    return jnp.cos(ang), jnp.sin(ang)


def apply_rope(x, cos, sin):
    c = cos[None, :, None, :].astype(x.dtype)
    s = sin[None, :, None, :].astype(x.dtype)
    x1, x2 = jnp.split(x, 2, axis=-1)
    return jnp.concatenate([x1 * c - x2 * s, x2 * c + x1 * s], axis=-1)


def gmlp_mixer(uv, v_norm_g, w_s, b_s):
    Bsz, S, _ = uv.shape
    nc = S // CHUNK
    uv = jax.nn.gelu(uv)
    u, v = jnp.split(uv, 2, axis=-1)
    v = rmsnorm(v, v_norm_g).reshape(Bsz, nc, CHUNK, GMLP_HEADS, GMLP_HEAD_DIM)
    causal = jnp.tril(jnp.ones((CHUNK, CHUNK), dtype=w_s.dtype))
    mixed = jnp.einsum('hts,bcshd->bcthd', w_s * causal, v)
    mixed = mixed + b_s.T[None, None, :, :, None]
    return u * mixed.reshape(Bsz, S, GMLP_W)


def causal_depthwise_conv(x, w, b):
    K, C = w.shape
    out = lax.conv_general_dilated(
        x, w[:, None, :], window_strides=(1,), padding=((K - 1, 0),),
        dimension_numbers=('NWC', 'WIO', 'NWC'), feature_group_count=C)
    return out + b


def ssd_mixer(z, xbc, dt_raw, conv_w, conv_b, dt_bias, a_log, d_skip, norm_g):
    Bsz, S, _ = xbc.shape
    nc = S // CHUNK
    hpg = SSD_HEADS // SSD_GROUPS
    xbc = jax.nn.silu(causal_depthwise_conv(xbc, conv_w, conv_b))
    xs, Bm, Cm = jnp.split(xbc, [SSD_W, SSD_W + SSD_GROUPS * SSD_STATE], axis=-1)
    xs_c = xs.reshape(Bsz, nc, CHUNK, SSD_HEADS, SSD_HEAD_DIM)
    B_c = Bm.reshape(Bsz, nc, CHUNK, SSD_GROUPS, SSD_STATE)
    C_c = Cm.reshape(Bsz, nc, CHUNK, SSD_GROUPS, SSD_STATE)
    dt = jax.nn.softplus((dt_raw + dt_bias).astype(jnp.float32))
    A = -jnp.exp(a_log.astype(jnp.float32))
    dt_h = dt.reshape(Bsz, nc, CHUNK, SSD_HEADS).transpose(0, 1, 3, 2)
    a_h = jnp.cumsum(dt_h * A[None, None, :, None], axis=-1)
    causal = jnp.tril(jnp.ones((CHUNK, CHUNK), dtype=bool))
    seg = a_h[..., :, None] - a_h[..., None, :]
    decay = jnp.exp(jnp.where(causal, seg, -jnp.inf))
    cb = jnp.einsum('bctgn,bcsgn->bcgts', C_c, B_c)
    cb_h = jnp.repeat(cb, hpg, axis=2)
    w = cb_h * decay * dt_h[:, :, :, None, :]
    y_diag = jnp.einsum('bchts,bcshp->bcthp', w, xs_c)
    B_h = jnp.repeat(B_c, hpg, axis=3)
    C_h = jnp.repeat(C_c, hpg, axis=3)
    decay_to_end = jnp.exp(a_h[..., -1:] - a_h) * dt_h
    states = jnp.einsum('bchs,bcshn,bcshp->bchpn', decay_to_end, B_h, xs_c)
    chunk_decay = jnp.exp(a_h[..., -1])

    def step(h, inp):
        dec, st = inp
        return dec[:, :, None, None] * h + st, h

    h0 = jnp.zeros((Bsz, SSD_HEADS, SSD_HEAD_DIM, SSD_STATE), states.dtype)
    _, h_prev = lax.scan(step, h0, (chunk_decay.transpose(1, 0, 2), states.transpose(1, 0, 2, 3, 4)))
    h_prev = h_prev.transpose(1, 0, 2, 3, 4)
    y_off = jnp.einsum('bcthn,bchpn,bcht->bcthp', C_h, h_prev, jnp.exp(a_h))
    y = y_diag + y_off + xs_c * d_skip[:, None]
    y = y.reshape(Bsz, S, SSD_W).astype(z.dtype)
    yg = (y * jax.nn.silu(z)).reshape(Bsz, S, SSD_GROUPS, SSD_W // SSD_GROUPS)
    return rmsnorm(yg, norm_g.reshape(SSD_GROUPS, -1)).reshape(Bsz, S, SSD_W)


def causal_block_attention(q, k, v, scale):
    Bsz, S, H, Dqk = q.shape
    nq = S // CHUNK
    qb = q.reshape(Bsz, nq, CHUNK, H, Dqk).transpose(1, 0, 3, 2, 4)
    k_pos = jnp.arange(S)

    def one_block(args):
        q_blk, blk = args
        s = jnp.einsum('bhqd,bkhd->bhqk', q_blk, k).astype(jnp.float32) * scale
        q_pos = blk * CHUNK + jnp.arange(CHUNK)
        s = jnp.where(k_pos[None, :] <= q_pos[:, None], s, -jnp.inf)
        p = jax.nn.softmax(s, axis=-1).astype(v.dtype)
        return jnp.einsum('bhqk,bkhd->bqhd', p, v)

    out = lax.map(one_block, (qb, jnp.arange(nq)))
    return out.transpose(1, 0, 2, 3, 4).reshape(Bsz, S, H, v.shape[-1])


def mla_mixer(c_q, c_kv, k_rope_raw, q_norm_g, w_qb, kv_norm_g, w_kvb, cos, sin):
    Bsz, S, _ = c_q.shape
    q = (rmsnorm(c_q, q_norm_g) @ w_qb).reshape(Bsz, S, MLA_HEADS, MLA_NOPE + MLA_ROPE)
    kv = (rmsnorm(c_kv, kv_norm_g) @ w_kvb).reshape(Bsz, S, MLA_HEADS, MLA_NOPE + MLA_V)
    q_nope, q_rope = jnp.split(q, [MLA_NOPE], axis=-1)
    k_nope, v = jnp.split(kv, [MLA_NOPE], axis=-1)
    q_rope = apply_rope(q_rope, cos, sin)
    k_rope = apply_rope(k_rope_raw[:, :, None, :], cos, sin)
    q = jnp.concatenate([q_nope, q_rope], axis=-1)
    k = jnp.concatenate([k_nope, jnp.broadcast_to(k_rope, (Bsz, S, MLA_HEADS, MLA_ROPE))], axis=-1)
    scale = 1.0 / math.sqrt(MLA_NOPE + MLA_ROPE)
    out = causal_block_attention(q, k, v, scale)
    return out.reshape(Bsz, S, MLA_W)


def squared_relu_mlp(h, w1, w2):
    return jnp.square(jax.nn.relu(h @ w1)) @ w2


def setup_inputs(seed: int = 0) -> dict:
    key = jax.random.key(seed)
    ks = jax.random.split(key, 21)

    def nrm(k, shape, scale):
        return jax.random.normal(k, shape, jnp.float32) * scale

    def gain(k, shape):
        return 1.0 + 0.02 * jax.random.normal(k, shape, jnp.float32)

    dt0 = jnp.exp(jax.random.uniform(ks[8], (DEPTH, SSD_HEADS), jnp.float32,
                                     math.log(1e-3), math.log(1e-1)))
    dt_bias = dt0 + jnp.log(-jnp.expm1(-dt0))
    a_log = jnp.log(jax.random.uniform(ks[9], (DEPTH, SSD_HEADS), jnp.float32, 1.0, 16.0))
    return {
        "x": nrm(ks[0], (BATCH, SEQ, D_MODEL), 1.0),
        "norm_mix_g": gain(ks[1], (DEPTH, D_MODEL)),
        "w_in": nrm(ks[2], (DEPTH, D_MODEL, D_IN), D_MODEL ** -0.5),
        "gmlp_v_norm_g": gain(ks[3], (DEPTH, GMLP_W)),
        "gmlp_w_s": nrm(ks[4], (DEPTH, GMLP_HEADS, CHUNK, CHUNK), CHUNK ** -0.5),
        "gmlp_b_s": gain(ks[5], (DEPTH, GMLP_HEADS, CHUNK)),
        "ssd_conv_w": nrm(ks[6], (DEPTH, SSD_CONV, SSD_XBC), SSD_CONV ** -0.5),
        "ssd_conv_b": nrm(ks[7], (DEPTH, SSD_XBC), 0.02),
        "ssd_dt_bias": dt_bias,
        "ssd_a_log": a_log,
        "ssd_d": 1.0 + 0.1 * jax.random.normal(ks[10], (DEPTH, SSD_HEADS), jnp.float32),
        "ssd_norm_g": gain(ks[11], (DEPTH, SSD_W)),
        "mla_q_norm_g": gain(ks[12], (DEPTH, MLA_Q_RANK)),
        "mla_w_qb": nrm(ks[13], (DEPTH, MLA_Q_RANK, MLA_HEADS * (MLA_NOPE + MLA_ROPE)), MLA_Q_RANK ** -0.5),
        "mla_kv_norm_g": gain(ks[14], (DEPTH, MLA_KV_RANK)),
        "mla_w_kvb": nrm(ks[15], (DEPTH, MLA_KV_RANK, MLA_HEADS * (MLA_NOPE + MLA_V)), MLA_KV_RANK ** -0.5),
        "w_out": nrm(ks[16], (DEPTH, D_MIX, D_MODEL), D_MIX ** -0.5),
        "norm_mlp_g": gain(ks[17], (DEPTH, D_MODEL)),
        "mlp_w1": nrm(ks[18], (DEPTH, D_MODEL, D_FF), D_MODEL ** -0.5),
        "mlp_w2": nrm(ks[19], (DEPTH, D_FF, D_MODEL), D_FF ** -0.5),
        "final_norm_g": gain(ks[20], (D_MODEL,)),
    }


def reference(x, norm_mix_g, w_in, gmlp_v_norm_g, gmlp_w_s, gmlp_b_s, ssd_conv_w, ssd_conv_b,
              ssd_dt_bias, ssd_a_log, ssd_d, ssd_norm_g, mla_q_norm_g, mla_w_qb, mla_kv_norm_g,
              mla_w_kvb, w_out, norm_mlp_g, mlp_w1, mlp_w2, final_norm_g):
    S = x.shape[1]
    cos, sin = rope_tables(S)
    for l in range(DEPTH):
        h = rmsnorm(x, norm_mix_g[l])
        uv, z, xbc, dt_raw, c_q, c_kv, k_rope = split_cols(h @ w_in[l])
        y_a = gmlp_mixer(uv, gmlp_v_norm_g[l], gmlp_w_s[l], gmlp_b_s[l])
        y_b = ssd_mixer(z, xbc, dt_raw, ssd_conv_w[l], ssd_conv_b[l], ssd_dt_bias[l],
                        ssd_a_log[l], ssd_d[l], ssd_norm_g[l])
        y_c = mla_mixer(c_q, c_kv, k_rope, mla_q_norm_g[l], mla_w_qb[l], mla_kv_norm_g[l],
                        mla_w_kvb[l], cos, sin)
        mix = jnp.concatenate([y_a, y_b, y_c], axis=-1)
        x = x + mix @ w_out[l]
        x = x + squared_relu_mlp(rmsnorm(x, norm_mlp_g[l]), mlp_w1[l], mlp_w2[l])
    return rmsnorm(x, final_norm_g)
```

```python
import math
import os
from contextlib import ExitStack

import numpy as np
import ml_dtypes

import concourse.bass as bass
import concourse.mybir as mybir
from concourse.bass_utils import run_bass_kernel_spmd

F32 = mybir.dt.float32
BF16 = mybir.dt.bfloat16
ALU = mybir.AluOpType
AF = mybir.ActivationFunctionType

NCORES = 8
D = 1024
SEQ = 16384
DEPTH = 4
TL = 2048
NCL = 16
EPS = 1e-6
D_IN = 2214
V0, Z0, XBC0, DT0, CQ0, CKV0, KR0 = 256, 512, 896, 1536, 1542, 1926, 2182
WIN_N = D_IN + 32
OFF_WIN = 0
OFF_WQ = OFF_WIN + 8 * WIN_N
OFF_WKV = OFF_WQ + 3 * 1152
OFF_WOUT = OFF_WKV + 2 * 768
OFF_W1 = OFF_WOUT + 8 * 1024
OFF_W2 = OFF_W1 + 32768
NW = OFF_W2 + 32768
NWS = NW // NCORES
C_GMIX, C_GMLP, C_GQ, C_GKV, C_CW, C_CB, C_GFIN, NCOLS = 0, 8, 16, 19, 21, 41, 46, 54
R_GV, R_NG, R_DSK, R_DTB, R_ALOG, NROWS = 0, 256, 640, 1024, 1030, 1036
ATT_SCALE = 1.0 / math.sqrt(96.0)

COMPUTE = ("pe", "act", "dve", "pool", "sp")
QUEUES = ("pe", "act", "dve", "pool", "sp")


class Buf:
    __slots__ = ("name", "t", "w", "r", "dsem", "dcnt")

    def __init__(self, name, t=None):
        self.name = name
        self.t = t
        self.w = []
        self.r = []
        self.dsem = None
        self.dcnt = 0

    def __getitem__(self, idx):
        return self.t[idx]


def _compress(deps):
    m = {}
    for k, v in deps:
        if m.get(k, 0) < v:
            m[k] = v
    return list(m.items())


class Sched:
    def __init__(self, nc):
        self.nc = nc
        self.ops = {q: [] for q in QUEUES}
        self.known = {q: {} for q in QUEUES}
        self.dbufs = []
        self.final_waits = []
        self.epoch = None
        self.es = ExitStack()
        self.scopes = []

    def _stack(self):
        return self.scopes[-1] if self.scopes else self.es

    def sb(self, name, shape, dtype, n=1):
        t = self._stack().enter_context(self.nc.sbuf_tensor("sb_" + name, list(shape), dtype))
        if n == 1:
            return Buf(name, t)
        return [Buf("%s%d" % (name, i), t) for i in range(n)]

    def ps(self, name, shape, dtype):
        t = self._stack().enter_context(self.nc.psum_tensor("ps_" + name, list(shape), dtype))
        return Buf(name, t)

    def push_scope(self):
        self.scopes.append(ExitStack())

    def pop_scope(self):
        self.barrier_all()
        self.scopes.pop().close()

    def _collect(self, q, reads, writes):
        deps = {}

        def add(d):
            k, v = d
            if k == "pe" and q == "pe":
                return
            if deps.get(k, 0) < v:
                deps[k] = v
        if self.epoch is not None and q != "sp":
            add(self.epoch)
        for b in reads:
            for d in b.w:
                add(d)
        for b in writes:
            for d in b.w:
                add(d)
            for d in b.r:
                add(d)
        out = []
        kn = self.known[q]
        for k, v in deps.items():
            if kn.get(k, 0) < v:
                kn[k] = v
                out.append((k, v))
        return out

    def _commit(self, dep, reads, writes):
        for b in reads:
            if b not in writes:
                b.r.append(dep)
                if len(b.r) > 16:
                    b.r = _compress(b.r)
        for b in writes:
            b.w = [dep]
            b.r = []

    def op(self, q, fn, reads=(), writes=()):
        reads = list(reads)
        writes = list(writes)
        waits = self._collect(q, reads, writes)
        seq = len(self.ops[q]) + 1
        self.ops[q].append(dict(fn=fn, waits=waits, seq=seq, dma=None))
        self._commit((q, seq), reads, writes)

    def dma(self, q, out_ap, in_ap, sb, reads=(), writes=(), **kw):
        reads = list(reads)
        writes = list(writes)
        waits = self._collect(q, reads, writes)
        if sb.dsem is None:
            sb.dsem = ("d", len(self.dbufs))
            self.dbufs.append(sb)
        sb.dcnt += 1
        dep = (sb.dsem, sb.dcnt * 16)
        seq = len(self.ops[q]) + 1

        def fn(eng):
            return eng.dma_start(out=out_ap, in_=in_ap, **kw)
        self.ops[q].append(dict(fn=fn, waits=waits, seq=seq, dma=sb.dsem))
        self._commit(dep, reads, writes)

    def custom16(self, q, fn, sem_buf, reads=(), writes=()):
        reads = list(reads)
        writes = list(writes)
        waits = self._collect(q, reads, writes)
        if sem_buf.dsem is None:
            sem_buf.dsem = ("d", len(self.dbufs))
            self.dbufs.append(sem_buf)
        sem_buf.dcnt += 1
        dep = (sem_buf.dsem, sem_buf.dcnt * 16)
        seq = len(self.ops[q]) + 1
        self.ops[q].append(dict(fn=fn, waits=waits, seq=seq, dma=sem_buf.dsem))
        self._commit(dep, reads, writes)

    def _outstanding(self):
        deps = []
        for q in COMPUTE:
            for o in reversed(self.ops[q]):
                if o["dma"] is None:
                    deps.append((q, o["seq"]))
                    break
        for b in self.dbufs:
            if b.dcnt:
                deps.append((b.dsem, b.dcnt * 16))
        return deps

    def barrier_all(self):
        deps = self._outstanding()
        kn = self.known["sp"]
        waits = []
        for k, v in deps:
            if k == "sp":
                continue
            if kn.get(k, 0) < v:
                kn[k] = v
                waits.append((k, v))
        seq = len(self.ops["sp"]) + 1
        self.ops["sp"].append(dict(fn=lambda e: e.nop(), waits=waits, seq=seq, dma=None))
        self.epoch = ("sp", seq)

    def finish(self, bufs=()):
        self.final_waits.extend(self._outstanding())

    def emit(self):
        nc = self.nc
        fw = {}
        for k, v in self.final_waits:
            if k != "sp":
                fw[k] = max(fw.get(k, 0), v)
        final = list(fw.items())
        waited = {q: set() for q in COMPUTE}
        for q in QUEUES:
            for o in self.ops[q]:
                for k, v in o["waits"]:
                    if k in waited:
                        waited[k].add(v)
        for k, v in final:
            if k in waited:
                waited[k].add(v)
        cmap = {q: {v: i + 1 for i, v in enumerate(sorted(waited[q]))} for q in COMPUTE}
        with ExitStack() as es:
            sems = {}
            for q in COMPUTE:
                sems[q] = es.enter_context(nc.semaphore("s_" + q))
            for i in range(len(self.dbufs)):
                sems[("d", i)] = es.enter_context(nc.semaphore("d%d" % i))
            block = es.enter_context(nc.Block())

            def replay(q, eng, extra_final=False):
                for o in self.ops[q]:
                    for k, v in o["waits"]:
                        eng.wait_ge(sems[k], cmap[k][v] if k in cmap else v)
                    ins = o["fn"](eng)
                    if o["dma"] is not None:
                        ins.then_inc(sems[o["dma"]], 16)
                    elif o["seq"] in cmap[q]:
                        ins.then_inc(sems[q], 1)
                if extra_final:
                    for k, v in final:
                        eng.wait_ge(sems[k], cmap[k][v] if k in cmap else v)

            @block.tensor
            def _(e):
                replay("pe", e)

            @block.scalar
            def _(e):
                replay("act", e)

            @block.vector
            def _(e):
                replay("dve", e)

            @block.gpsimd
            def _(e):
                replay("pool", e)

            @block.sync
            def _(e):
                replay("sp", e, extra_final=True)
        self.es.close()


class K:
    def __init__(self, nc):
        self.nc = nc
        self.S = Sched(nc)
        self.banks = None
        self.bi = 0
        self.ve = 0

    def setup_psum(self):
        S = self.S
        self.banks = [S.ps("pb%d" % i, [128, 512], F32) for i in range(7)]
        self.ptb = None

    def bank(self):
        b = self.banks[self.bi % len(self.banks)]
        self.bi += 1
        return b

    def veng(self):
        self.ve += 1
        return "dve" if self.ve % 2 else "pool"

    def mm(self, out, lhsT, rhs, start=True, stop=True, r=(), w=()):
        self.S.op("pe", lambda e: e.matmul(out, lhsT=lhsT, rhs=rhs, start=start, stop=stop), r, w)

    def tr(self, out, in_, ident, r=(), w=()):
        self.S.op("pe", lambda e: e.transpose(out=out, in_=in_, identity=ident), r, w)

    def act(self, out, in_, func, r=(), w=(), bias=None, scale=1.0, accum=None):
        kw = {}
        if bias is not None:
            kw["bias"] = bias
        if accum is not None:
            kw["accum_out"] = accum
        self.S.op("act", lambda e: e.activation(out=out, in_=in_, func=func, scale=scale, **kw), r, w)

    def tt(self, q, out, in0, in1, op, r=(), w=()):
        self.S.op(q, lambda e: e.tensor_tensor(out=out, in0=in0, in1=in1, op=op), r, w)

    def ts(self, q, out, in0, s1, op0, r=(), w=(), s2=None, op1=None):
        if op1 is None:
            self.S.op(q, lambda e: e.tensor_scalar(out=out, in0=in0, scalar1=s1, scalar2=None, op0=op0), r, w)
        else:
            self.S.op(q, lambda e: e.tensor_scalar(out=out, in0=in0, scalar1=s1, scalar2=s2, op0=op0, op1=op1), r, w)

    def stt(self, q, out, in0, scalar, in1, op0, op1, r=(), w=()):
        q = "dve"
        self.S.op(q, lambda e: e.scalar_tensor_tensor(out=out, in0=in0, scalar=scalar, in1=in1, op0=op0, op1=op1), r, w)

    def cp(self, q, out, in_, r=(), w=()):
        if q == "act":
            self.S.op(q, lambda e: e.copy(out=out, in_=in_), r, w)
        else:
            self.S.op(q, lambda e: e.tensor_copy(out=out, in_=in_), r, w)

    def memset(self, q, out, val, w=()):
        self.S.op(q, lambda e: e.memset(out, val), (), w)

    def recip(self, out, in_, r=(), w=()):
        self.S.op("dve", lambda e: e.reciprocal(out=out, in_=in_), r, w)

    def ld(self, out, in_, buf, q="sp", **kw):
        self.S.dma(q, out, in_, buf, reads=(), writes=[buf], **kw)

    def st(self, out, in_, buf, q="sp", **kw):
        self.S.dma(q, out, in_, buf, reads=[buf], writes=(), **kw)

    def wload(self, dst_flat, buf, wb, n0, n1):
        n = n0
        while n < n1:
            r = n // NWS
            c0 = n % NWS
            ln = min(n1 - n, NWS - c0)
            self.ld(dst_flat[:, n - n0:n - n0 + ln], wb[r, :, c0:c0 + ln], buf)
            n += ln


class StopEmit(Exception):
    pass


STOP = float(os.environ.get("MK_STOP", "1000000"))


def ckpt(n):
    if n >= STOP:
        raise StopEmit()


def flat(buf):
    t = buf.t
    nd = len(t.shape)
    if nd == 2:
        return t[:, :]
    if nd == 3:
        return t[:, :, :].rearrange("p a b -> p (a b)")
    if nd == 4:
        return t[:, :, :, :].rearrange("p a b c -> p (a b c)")
    raise ValueError


def emit_w(k, wl, wbs):
    S = k.S
    PIECE = 2048
    bufs_f = [S.sb("wf%d" % i, [128, PIECE], F32) for i in range(3)]
    bufs_b = [S.sb("wc%d" % i, [128, PIECE], BF16) for i in range(3)]
    i = 0
    for l in range(DEPTH):
        n = 0
        while n < NWS:
            ln = min(PIECE, NWS - n)
            f = bufs_f[i % 3]
            b = bufs_b[i % 3]
            k.ld(f[:, 0:ln], wl[l * 128:(l + 1) * 128, n:n + ln], f)
            q = ("dve", "pool", "act")[i % 3]
            k.cp(q, b[:, 0:ln], f[:, 0:ln], [f], [b])
            k.st(wbs[l * 128:(l + 1) * 128, n:n + ln], b[:, 0:ln], b)
            n += ln
            i += 1


def emit_p1(k, io):
    S = k.S
    S.push_scope()
    win = S.sb("win", [128, 8, WIN_N], BF16)
    wq = S.sb("wq", [128, 3, 1152], BF16)
    wkv = S.sb("wkv", [128, 2, 768], BF16)
    k.wload(flat(win), win, io["wb"], OFF_WIN, OFF_WIN + 8 * WIN_N)
    k.wload(flat(wq), wq, io["wb"], OFF_WQ, OFF_WQ + 3 * 1152)
    k.wload(flat(wkv), wkv, io["wb"], OFF_WKV, OFF_WKV + 2 * 768)
    cols = S.sb("cols", [128, NCOLS], F32)
    rows = S.sb("rows", [128, NROWS], F32)
    k.ld(cols[:, :], io["cols"][:, :], cols)
    k.ld(rows[:, :], io["rows"][:, :], rows)
    identf = S.sb("identf", [128, 128], F32)
    triu = S.sb("triu", [128, 128], F32)
    k.ld(identf[:, :], io["ident"][:, :], identf)
    k.ld(triu[:, :], io["triu"][:, :], triu)
    identb = S.sb("identb", [128, 128], BF16)
    k.cp("dve", identb[:, :], identf[:, :], [identf], [identb])
    onesb = S.sb("onesb", [128, 128], BF16)
    k.memset("pool", onesb[:, :], 1.0, [onesb])
    onesf = S.sb("onesf", [128, 128], F32)
    k.memset("pool", onesf[:, :], 1.0, [onesf])
    wsf = S.sb("wsf", [128, 4, 128], F32)
    k.ld(wsf[:, :, :], io["wsT"][:, :, :], wsf)
    wsb = S.sb("wsb", [128, 4, 128], BF16)
    k.tt("dve", wsb[:, :, :], wsf[:, :, :], triu[:, :].unsqueeze(1).to_broadcast([128, 4, 128]), ALU.mult,
         [wsf, triu], [wsb])
    bsf = S.sb("bsf", [1, 512], F32)
    k.ld(bsf[:, :], io["bsr"][:, :], bsf)
    bsb = S.sb("bsb", [1, 512], BF16)
    k.cp("dve", bsb[:, :], bsf[:, :], [bsf], [bsb])
    abc_ = S.sb("Abc", [128, 6], F32)
    k.act(abc_[:, :], rows[:, R_ALOG:R_ALOG + 6], AF.Exp, [rows], [abc_])
    k.ts("dve", abc_[:, :], abc_[:, :], -1.0, ALU.mult, [abc_], [abc_])

    sqt = S.sb("sqt", [128, 8, 512], BF16)
    rst = S.sb("rst", [128, 512], F32)
    k.ptb = S.ps("ptb", [128, 1024], BF16)

    ckpt(1)

    def rmsnorm_fm(xt, nk, ntok, gcol0, n_feat, outb, tag):
        sq = sqt
        rs = rst
        k.act(sq[:, 0:nk, 0:ntok], xt[:, :, :], AF.Square, [xt], [sq])
        bk = k.bank()
        for kc in range(nk):
            k.mm(bk[:, 0:ntok], onesb[:, :], sq[:, kc, 0:ntok], kc == 0, kc == nk - 1, [onesb, sq], [bk])
        k.act(rs[:, 0:ntok], bk[:, 0:ntok], AF.Ln, [bk], [rs], bias=EPS, scale=1.0 / n_feat)
        k.act(rs[:, 0:ntok], rs[:, 0:ntok], AF.Exp, [rs], [rs], scale=-0.5)
        for kc in range(nk):
            k.stt(k.veng(), outb[:, kc, :], xt[:, kc, :], cols[:, gcol0 + kc:gcol0 + kc + 1], rs[:, 0:ntok],
                  ALU.mult, ALU.mult, [xt, cols, rs], [outb])

    xh = S.sb("xh", [128, 8, 48], F32)
    k.ld(xh[:, :, :], io["xh"].rearrange("k p t -> p k t"), xh)
    hh = S.sb("hh", [128, 8, 48], BF16)
    rmsnorm_fm(xh, 8, 48, C_GMIX, 1024.0, hh, "h")
    xbch = S.sb("xbch", [128, 5, 48], F32)
    for b in range(5):
        bk = k.bank()
        for kc in range(8):
            k.mm(bk[:, 0:48], win[:, kc, XBC0 + b * 128:XBC0 + (b + 1) * 128], hh[:, kc, :], kc == 0, kc == 7,
                 [win, hh], [bk])
        k.cp("act", xbch[:, b, :], bk[:, 0:48], [bk], [xbch])

    ckpt(2)
    xT = [S.sb("xT%d" % i, [128, 8, 512], F32) for i in range(1)]
    hT = S.sb("hT", [128, 8, 512], BF16)
    ug = S.sb("ug", [128, 2, 512], BF16)
    vg = S.sb("vg", [128, 256], F32)
    vjunk = S.sb("vjunk", [128, 384], F32)
    vss = S.sb("vss", [128, 2], F32)
    vn = S.sb("vn", [128, 4, 256], BF16)
    ya = S.sb("ya", [128, 2, 512], BF16)
    zs = S.sb("zs", [128, 4, 384], F32)
    dtt = S.sb("dtt", [128, 4, 6], F32)
    xbc = S.sb("xbc", [128, 5, 4, 131], F32)
    cq = S.sb("cq", [128, 3, 512], F32)
    cqn = S.sb("cqn", [128, 3, 512], BF16)
    ckv = S.sb("ckv", [128, 2, 512], F32)
    ckvn = S.sb("ckvn", [128, 2, 512], BF16)
    cs = S.sb("cs", [96, 2, 512], F32)
    cs32 = S.sb("cs32", [32, 2, 512], F32)
    q1 = S.sb("q1", [96, 512], F32)
    q2 = S.sb("q2", [96, 512], F32)
    qt = S.sb("qt", [96, 6, 512], BF16)
    ktt = S.sb("ktt", [128, 3, 512], BF16)
    krr = S.sb("krr", [32, 512], BF16)
    vat = S.sb("vat", [128, 4, 6, 65], BF16)
    k.memset("pool", vat[:, :, :, 64:65], 1.0, [vat])
    cv = S.sb("cv", [128, 5, 512], BF16)
    acc = [S.sb("acc%d" % i, [128, 4, 128], F32) for i in range(2)]
    ctt = S.sb("ctt", [64, 2, 512], BF16)
    cz = S.sb("cz", [128, 2, 512], BF16)
    k.memset("pool", cz[:, :, :], 0.0, [cz])
    xsb = S.sb("xsb", [128, 512], BF16)
    dta = S.sb("dta", [128, 6], F32)
    asb = S.sb("asb", [128, 6], F32)
    eat = S.sb("eat", [128, 4, 6], F32)
    cbm = S.sb("cbm", [128, 2, 128], F32)
    rep = [S.sb("rep%d" % i, [128, 128], F32) for i in range(2)]
    E = S.sb("E", [128, 6, 128], F32)
    dte = S.sb("dte", [128, 6], F32)
    WT = S.sb("WT", [128, 6, 128], BF16)
    ydt = S.sb("ydt", [128, 4, 384], F32)
    ytmp = S.sb("ytmp", [128, 384], F32)
    xsd = S.sb("xsd", [128, 384], BF16)
    stt_ = S.sb("stt", [64, 4, 384], F32)

    for T in range(4):
        t0 = T * 512
        x = xT[0]
        k.ld(x[:, :, :], io["xr"][:, :, t0:t0 + 512].rearrange("k p t -> p k t"), x)
        k.ld(cs[:, :, :], io["cs96"][:, :, t0:t0 + 512], cs)
        k.ld(cs32[:, :, :], io["cs96"][64:96, :, t0:t0 + 512], cs32)
        ckpt(3)
        rmsnorm_fm(x, 8, 512, C_GMIX, 1024.0, hT, "x")
        ckpt(4)

        def fm(col0, width):
            bk = k.bank()
            for kc in range(8):
                k.mm(bk[0:width, :], win[:, kc, col0:col0 + width], hT[:, kc, :], kc == 0, kc == 7, [win, hT], [bk])
            return bk
        for b in range(2):
            bk = fm(b * 128, 128)
            k.act(ug[:, b, :], bk[:, :], AF.Gelu_apprx_tanh, [bk], [ug])
        for b in range(5):
            bk = fm(XBC0 + b * 128, 128)
            k.cp("act", xbc[:, b, :, 3:131], bk[:, :].rearrange("p (c t) -> p c t", c=4), [bk], [xbc])
            k.cp("pool", xbc[:, b, :, 0:3],
                 xbch[:, b, T * 12:(T + 1) * 12].rearrange("p (c t) -> p c t", c=4), [xbch], [xbc])
        for b in range(3):
            bk = fm(CQ0 + b * 128, 128)
            k.cp("act", cq[:, b, :], bk[:, :], [bk], [cq])
        for b in range(2):
            bk = fm(CKV0 + b * 128, 128)
            k.cp("act", ckv[:, b, :], bk[:, :], [bk], [ckv])
        bk1 = fm(KR0, 32)
        bk2 = fm(D_IN, 32)
        k.tt("dve", q1[0:32, :], bk1[0:32, :], cs32[:, 0, :], ALU.mult, [bk1, cs32], [q1])
        k.tt("dve", q2[0:32, :], bk2[0:32, :], cs32[:, 1, :], ALU.mult, [bk2, cs32], [q2])
        k.tt("pool", krr[:, :], q1[0:32, :], q2[0:32, :], ALU.add, [q1, q2], [krr])
        k.st(io["KT"][384:416, t0:t0 + 512], krr[:, :], krr)
        ckpt(5)
        for cl in range(4):
            tk = slice(cl * 128, (cl + 1) * 128)
            bkv = k.bank()
            for kc in range(8):
                k.mm(bkv[:, 0:256], hT[:, kc, tk], win[:, kc, V0:V0 + 256], kc == 0, kc == 7, [win, hT], [bkv])
            for kc in range(8):
                k.mm(bkv[:, 256:262], hT[:, kc, tk], win[:, kc, DT0:DT0 + 6], kc == 0, kc == 7, [win, hT], [bkv])
            bkz = k.bank()
            for kc in range(8):
                k.mm(bkz[:, 0:384], hT[:, kc, tk], win[:, kc, Z0:Z0 + 384], kc == 0, kc == 7, [win, hT], [bkz])
            k.act(vg[:, :], bkv[:, 0:256], AF.Gelu_apprx_tanh, [bkv], [vg])
            k.act(vjunk[:, 0:256], vg[:, :], AF.Square, [vg], [vjunk, vss], accum=vss[:, 0:1])
            k.act(vss[:, 1:2], vss[:, 0:1], AF.Ln, [vss], [vss], bias=EPS, scale=1.0 / 256)
            k.act(vss[:, 1:2], vss[:, 1:2], AF.Exp, [vss], [vss], scale=-0.5)
            k.stt("dve", vn[:, cl, :], vg[:, :], vss[:, 1:2], rows[:, R_GV:R_GV + 256], ALU.mult, ALU.mult,
                  [vg, vss, rows], [vn])
            k.act(zs[:, cl, :], bkz[:, 0:384], AF.Silu, [bkz], [zs])
            k.tt("dve", dtt[:, cl, :], bkv[:, 256:262], rows[:, R_DTB:R_DTB + 6], ALU.add, [bkv, rows], [dtt])
        k.act(dtt[:, :, :], dtt[:, :, :], AF.Exp, [dtt], [dtt])
        k.act(dtt[:, :, :], dtt[:, :, :], AF.Ln, [dtt], [dtt], bias=1.0)
        k.st(io["ZS"][:, T * 4 * 384:(T + 1) * 4 * 384], flat(zs), zs)
        ckpt(6)
        for hp in range(2):
            bk = k.bank()
            for cl in range(4):
                for hx in range(2):
                    h = hp * 2 + hx
                    o = bk[hx * 64:(hx + 1) * 64, cl * 128:(cl + 1) * 128]
                    k.mm(o, vn[:, cl, h * 64:(h + 1) * 64], wsb[:, h, :], True, False, [vn, wsb], [bk])
                    k.mm(o, onesb[0:1, 0:64], bsb[0:1, h * 128:(h + 1) * 128], False, True, [onesb, bsb], [bk])
            k.tt("dve", ya[:, hp, :], ug[:, hp, :], bk[:, :], ALU.mult, [ug, bk], [ya])
        k.st(io["YA"][:, :, t0:t0 + 512], ya[:, :, :], ya)
        ckpt(7)
        rmsnorm_fm(cq, 3, 512, C_GQ, 384.0, cqn, "q")
        for h in range(6):
            bka = k.bank()
            for kc in range(3):
                k.mm(bka[0:96, :], wq[:, kc, h * 96:(h + 1) * 96], cqn[:, kc, :], kc == 0, kc == 2, [wq, cqn], [bka])
            bkb = k.bank()
            for kc in range(3):
                k.mm(bkb[0:96, :], wq[:, kc, 576 + h * 96:576 + (h + 1) * 96], cqn[:, kc, :], kc == 0, kc == 2,
                     [wq, cqn], [bkb])
            k.tt("dve", q1[:, :], bka[0:96, :], cs[:, 0, :], ALU.mult, [bka, cs], [q1])
            k.tt("dve", q2[:, :], bkb[0:96, :], cs[:, 1, :], ALU.mult, [bkb, cs], [q2])
            k.tt("pool", qt[:, h, :], q1[:, :], q2[:, :], ALU.add, [q1, q2], [qt])
        k.st(io["QT"][:, :, t0:t0 + 512], qt[:, :, :], qt)
        ckpt(8)
        rmsnorm_fm(ckv, 2, 512, C_GKV, 256.0, ckvn, "k")
        for hp in range(3):
            bk = k.bank()
            for hx in range(2):
                h = hp * 2 + hx
                for kc in range(2):
                    k.mm(bk[hx * 64:(hx + 1) * 64, :], wkv[:, kc, h * 64:(h + 1) * 64], ckvn[:, kc, :], kc == 0, kc == 1,
                         [wkv, ckvn], [bk])
            k.cp("act", ktt[:, hp, :], bk[:, :], [bk], [ktt])
        k.st(io["KT"][0:384, t0:t0 + 512].rearrange("(a p) t -> p a t", p=128), ktt[:, :, :], ktt)
        for cl in range(4):
            bk = k.bank()
            for kc in range(2):
                k.mm(bk[:, 0:384], ckvn[:, kc, cl * 128:(cl + 1) * 128], wkv[:, kc, 384:768], kc == 0, kc == 1,
                     [wkv, ckvn], [bk])
            k.cp("act", vat[:, cl, :, 0:64], bk[:, 0:384].rearrange("p (h d) -> p h d", h=6), [bk], [vat])
        k.st(io["VA"][:, T * 4 * 390:(T + 1) * 4 * 390], flat(vat), vat)
        ckpt(9)
        for b in range(5):
            a_ = acc[b % 2]
            q = "dve"
            cw = C_CW + b * 4
            k.ts(q, a_[:, :, :], xbc[:, b, :, 0:128], cols[:, cw:cw + 1], ALU.mult, [xbc, cols], [a_])
            for j in range(1, 4):
                k.stt(q, a_[:, :, :], xbc[:, b, :, j:j + 128], cols[:, cw + j:cw + j + 1], a_[:, :, :], ALU.mult, ALU.add,
                      [xbc, cols, a_], [a_])
            k.act(cv[:, b, :], a_[:, :, :].rearrange("p c t -> p (c t)"), AF.Silu, [a_, cols], [cv],
                  bias=cols[:, C_CB + b:C_CB + b + 1])
        k.cp("pool", cz[0:64, 0, :], cv[0:64, 4, :], [cv], [cz])
        k.cp("pool", cz[64:128, 1, :], cv[64:128, 4, :], [cv], [cz])
        k.cp("dve", ctt[:, 0, :], cv[0:64, 4, :], [cv], [ctt])
        k.cp("dve", ctt[:, 1, :], cv[64:128, 4, :], [cv], [ctt])
        k.st(io["CT"][:, :, t0:t0 + 512], ctt[:, :, :], ctt)
        ckpt(10)
        for cl in range(4):
            c = T * 4 + cl
            tk = slice(cl * 128, (cl + 1) * 128)
            for b in range(4):
                k.tr(k.ptb[:, b * 128:(b + 1) * 128], cv[:, b, tk], identb[:, :], [cv, identb], [k.ptb])
            k.cp("act", xsb[:, :], k.ptb[:, 0:512], [k.ptb], [xsb])
            ckpt(10.1)
            k.tt("dve", dta[:, :], dtt[:, cl, :], abc_[:, :], ALU.mult, [dtt, abc_], [dta])
            bka = k.bank()
            k.mm(bka[:, 0:6], triu[:, :], dta[:, :], True, True, [triu, dta], [bka])
            k.cp("dve", asb[:, :], bka[:, 0:6], [bka], [asb])
            ckpt(10.2)
            k.act(eat[:, cl, :], asb[:, :], AF.Exp, [asb], [eat])
            k.st(io["AL"][0:1, c * 6:(c + 1) * 6], asb[127:128, :], asb)
            ckpt(10.3)
            bkc = k.bank()
            for g in range(2):
                k.mm(bkc[:, g * 128:(g + 1) * 128], cv[:, 3, tk], cz[:, g, tk], True, True, [cv, cz], [bkc])
            ckpt(10.35)
            k.tt("dve", cbm[:, :, :], bkc[:, 0:256].rearrange("p (g t) -> p g t", g=2),
                 triu[:, :].unsqueeze(1).to_broadcast([128, 2, 128]), ALU.mult, [bkc, triu], [cbm])
            ckpt(10.4)
            bke = [k.bank(), k.bank()]
            for h in range(6):
                rp = rep[h % 2]
                k.ts("pool", rp[:, :], onesf[:, :], dta[:, h:h + 1], ALU.mult, [onesf, dta], [rp])
                be = bke[h // 3]
                k.mm(be[:, (h % 3) * 128:(h % 3 + 1) * 128], rp[:, :], triu[:, :], True, True, [rp, triu], [be])
            for h in range(6):
                be = bke[h // 3]
                k.ts("dve", E[:, h, :], be[:, (h % 3) * 128:(h % 3 + 1) * 128], asb[:, h:h + 1], ALU.subtract, [be, asb], [E],
                     s2=0.0, op1=ALU.min)
            k.act(E[:, :, :], E[:, :, :], AF.Exp, [E], [E])
            ckpt(10.5)
            k.tt("dve", dte[:, :], E[:, :, 127], dtt[:, cl, :], ALU.mult, [E, dtt], [dte])
            ckpt(10.6)
            for h in range(6):
                k.stt(k.veng(), WT[:, h, :], E[:, h, :], dtt[:, cl, h:h + 1], cbm[:, h // 3, :], ALU.mult, ALU.mult,
                      [E, dtt, cbm], [WT])
            bky = k.bank()
            for h in range(6):
                k.mm(bky[:, h * 64:(h + 1) * 64], WT[:, h, :], xsb[:, h * 64:(h + 1) * 64], True, True, [WT, xsb], [bky])
            k.tt("pool", ytmp[:, :], xsb[:, 0:384], rows[:, R_DSK:R_DSK + 384], ALU.mult, [xsb, rows], [ytmp])
            k.tt("dve", ydt[:, cl, :], ytmp[:, :], bky[:, 0:384], ALU.add, [ytmp, bky], [ydt])
            ckpt(10.8)
            k.tt("pool", xsd[:, :].rearrange("p (h d) -> p h d", h=6), xsb[:, 0:384].rearrange("p (h d) -> p h d", h=6),
                 dte[:, :].unsqueeze(2).to_broadcast([128, 6, 64]), ALU.mult, [xsb, dte], [xsd])
            bks = k.bank()
            for g in range(2):
                k.mm(bks[0:64, g * 192:(g + 1) * 192], xsb[:, 384 + g * 64:384 + (g + 1) * 64], xsd[:, g * 192:(g + 1) * 192],
                     True, True, [xsb, xsd], [bks])
            k.cp("act", stt_[:, cl, :], bks[0:64, 0:384], [bks], [stt_])
        ckpt(11)
        k.st(io["YD"][:, T * 4 * 384:(T + 1) * 4 * 384], flat(ydt), ydt)
        k.st(io["ST"][:, T * 4 * 384:(T + 1) * 4 * 384], flat(stt_), stt_)
        k.st(io["EA"][:, T * 24:(T + 1) * 24], flat(eat), eat)
    S.pop_scope()


def emit_p2(k, io, last):
    S = k.S
    S.push_scope()
    cols = S.sb("cols2", [128, NCOLS], F32)
    rows = S.sb("rows2", [128, NROWS], F32)
    k.ld(cols[:, :], io["cols"][:, :], cols)
    k.ld(rows[:, :], io["rows"][:, :], rows)
    identf = S.sb("identf2", [128, 128], F32)
    k.ld(identf[:, :], io["ident"][:, :], identf)
    identb = S.sb("identb2", [128, 128], BF16)
    k.cp("dve", identb[:, :], identf[:, :], [identf], [identb])
    onesb = S.sb("onesb2", [128, 128], BF16)
    k.memset("pool", onesb[:, :], 1.0, [onesb])
    onesf = S.sb("onesf2", [128, 64], F32)
    k.memset("pool", onesf[:, :], 1.0, [onesf])
    sel = S.sb("sel", [128, 8], F32)
    k.ld(sel[:, :], io["sel"][:, :], sel)
    bmf = S.sb("bmf", [128, 8, 128], F32)
    k.ld(bmf[:, :, :], io["bandmask"][:, :, :], bmf)
    bm = S.sb("bm", [128, 8, 128], BF16)
    k.cp("dve", bm[:, :, :], bmf[:, :, :], [bmf], [bm])
    mixT = S.sb("mixT", [128, 8, TL], BF16, n=8)
    mt = mixT[0].t
    k.ld(mt[:, 0, :], io["YA"][:, 0, :], mixT[0])
    k.ld(mt[:, 1, :], io["YA"][:, 1, :], mixT[1])

    ckpt(20)
    S.push_scope()
    k.ptb = S.ps("ptb2", [128, 1024], BF16)
    alr = S.sb("alr", [1, 768], F32)
    k.ld(alr[:, :], io["ALg"].rearrange("r (o n) -> o (r n)", o=1), alr)
    k.act(alr[:, :], alr[:, :], AF.Exp, [alr], [alr])
    dall = S.sb("dall", [64, 768], F32)
    for j in range(2):
        bk = k.bank()
        k.mm(bk[0:64, 0:384], onesf[0:1, 0:64], alr[0:1, j * 384:(j + 1) * 384], True, True, [onesf, alr], [bk])
        k.cp("act", dall[:, j * 384:(j + 1) * 384], bk[0:64, 0:384], [bk], [dall])
    H = S.sb("H", [64, 384], F32)
    htmp = S.sb("htmp", [64, 384], F32)
    k.memset("pool", H[:, :], 0.0, [H])
    hsel = S.sb("hsel", [64, NCL, 384], F32, n=NCL)
    hs = hsel[0].t
    sbuf = [S.sb("sbuf%d" % i, [64, 8, 384], F32) for i in range(2)]
    hselb = S.sb("hselb", [64, NCL, 384], BF16, n=NCL)
    hsb = hselb[0].t
    for c in range(NCL):
        k.memset("pool", hs[:, c, :], 0.0, [hsel[c]])
        sb_ = sbuf[c % 2]
        k.ld(sb_[:, :, :], io["STg"][:, :, c * 384:(c + 1) * 384].rearrange("r p n -> p r n"), sb_)
        for r in range(8):
            k.ts("pool", htmp[:, :], H[:, :], sel[0:64, r:r + 1], ALU.mult, [H, sel], [htmp])
            k.tt("pool", hs[:, c, :], hs[:, c, :], htmp[:, :], ALU.add, [htmp, hsel[c]], [hsel[c]])
            dcol = (r * 16 + c) * 6
            k.tt("pool", H[:, :].rearrange("p (h d) -> p h d", h=6), H[:, :].rearrange("p (h d) -> p h d", h=6),
                 dall[:, dcol:dcol + 6].unsqueeze(2).to_broadcast([64, 6, 64]), ALU.mult, [H, dall], [H])
            k.tt("pool", H[:, :], H[:, :], sb_[:, r, :], ALU.add, [H, sb_], [H])
        k.cp("dve", hsb[:, c, :], hs[:, c, :], [hsel[c]], [hselb[c]])
    ckpt(21)
    ctt = S.sb("ctt2", [64, 2, TL], BF16)
    k.ld(ctt[:, :, :], io["CT"][:, :, :], ctt)
    ea = S.sb("ea2", [128, NCL * 6], F32)
    k.ld(ea[:, :], io["EA"][:, :], ea)
    ydb = [S.sb("ydb%d" % i, [128, 4, 384], F32) for i in range(2)]
    zsb = [S.sb("zsb%d" % i, [128, 4, 384], F32) for i in range(2)]
    yv = S.sb("yv", [128, 384], F32)
    yjunk = S.sb("yjunk", [128, 192], F32)
    yss = S.sb("yss", [128, 4], F32)
    yn = S.sb("yn", [128, 384], BF16)
    for c in range(NCL):
        T = c // 4
        cl = c % 4
        if cl == 0:
            k.ld(flat(ydb[T % 2]), io["YD"][:, T * 1536:(T + 1) * 1536], ydb[T % 2])
            k.ld(flat(zsb[T % 2]), io["ZS"][:, T * 1536:(T + 1) * 1536], zsb[T % 2])
        yd = ydb[T % 2]
        zz = zsb[T % 2]
        bk = k.bank()
        for g in range(2):
            k.mm(bk[:, g * 192:(g + 1) * 192], ctt[:, g, c * 128:(c + 1) * 128], hsb[:, c, g * 192:(g + 1) * 192], True, True,
                 [ctt, hselb[c]], [bk])
        k.tt("dve", yv[:, :].rearrange("p (h d) -> p h d", h=6), bk[:, 0:384].rearrange("p (h d) -> p h d", h=6),
             ea[:, c * 6:(c + 1) * 6].unsqueeze(2).to_broadcast([128, 6, 64]), ALU.mult, [bk, ea], [yv])
        k.tt("dve", yv[:, :], yv[:, :], yd[:, cl, :], ALU.add, [yv, yd], [yv])
        k.tt("dve", yv[:, :], yv[:, :], zz[:, cl, :], ALU.mult, [yv, zz], [yv])
        for g in range(2):
            k.act(yjunk[:, :], yv[:, g * 192:(g + 1) * 192], AF.Square, [yv], [yjunk, yss], accum=yss[:, g:g + 1])
        k.act(yss[:, 2:4], yss[:, 0:2], AF.Ln, [yss], [yss], bias=EPS, scale=1.0 / 192)
        k.act(yss[:, 2:4], yss[:, 2:4], AF.Exp, [yss], [yss], scale=-0.5)
        for g in range(2):
            k.stt("dve", yn[:, g * 192:(g + 1) * 192], yv[:, g * 192:(g + 1) * 192], yss[:, 2 + g:3 + g],
                  rows[:, R_NG + g * 192:R_NG + (g + 1) * 192], ALU.mult, ALU.mult, [yv, yss, rows], [yn])
        for b in range(3):
            k.tr(k.ptb[:, b * 128:(b + 1) * 128], yn[:, b * 128:(b + 1) * 128], identb[:, :], [yn, identb], [k.ptb])
        k.cp("act", mt[:, 2:5, c * 128:(c + 1) * 128], k.ptb[:, 0:384].rearrange("p (b t) -> p b t", b=3), [k.ptb],
             [mixT[2], mixT[3], mixT[4]])
    S.pop_scope()

    ckpt(22)
    S.push_scope()
    qT = S.sb("qT", [96, 6, TL], BF16)
    k.ld(qT[:, :, :], io["QT"][:, :, :], qT)
    kt = [S.sb("kt%d" % i, [96, 6, TL], BF16) for i in range(2)]
    vt = [S.sb("vt%d" % i, [128, NCL * 390], BF16) for i in range(2)]
    pt = [S.sb("pt%d" % i, [128, 512], BF16) for i in range(3)]
    rl = S.sb("rl", [65, 512], F32)
    k.memset("pool", rl[:, :], 0.0, [rl])
    sel65 = S.sb("sel65", [65, 64], F32)
    k.memset("pool", sel65[:, :], 0.0, [sel65])
    k.memset("pool", sel65[64:65, :], 1.0, [sel65])
    bcs = S.sb("bcs", [64, 512], F32)
    obank = k.banks[0:6]
    sbank = [k.banks[6], S.ps("pb7", [128, 512], F32)]
    step = 0
    pi = 0
    for G in range(4):
        nk = 4 * G + 4
        for r in range(8):
            kb = kt[step % 2]
            vb = vt[step % 2]
            step += 1
            k.ld(kb[0:64, :, 0:nk * 128], io["KTg"][r, 0:384, 0:nk * 128].rearrange("(h p) t -> p h t", p=64), kb)
            for h in range(6):
                k.ld(kb[64:96, h, 0:nk * 128], io["KTg"][r, 384:416, 0:nk * 128], kb)
            k.ld(vb[:, 0:nk * 390], io["VAg"][r, :, 0:nk * 390], vb)
            for h in range(6):
                ob = obank[h]
                for cp_ in range(nk):
                    kj = cp_ - 4 * G
                    q0 = max(kj, 0) * 128
                    sb_ = sbank[pi % 2]
                    p = pt[pi % 3]
                    pi += 1
                    k.mm(sb_[:, q0:512], kb[0:96, h, cp_ * 128:(cp_ + 1) * 128], qT[0:96, h, G * 512 + q0:(G + 1) * 512], True, True,
                         [kb, qT], [sb_])
                    k.act(p[:, q0:512], sb_[:, q0:512], AF.Exp, [sb_], [p], scale=ATT_SCALE)
                    if kj >= 0:
                        k.tt("pool", p[:, q0:q0 + 128], p[:, q0:q0 + 128], bm[:, r, :], ALU.mult, [p, bm], [p])
                        if q0 > 0:
                            k.memset("pool", p[:, 0:q0], 0.0, [p])
                    k.mm(ob[0:65, :], vb[:, cp_ * 390 + h * 65:cp_ * 390 + (h + 1) * 65], p[:, :],
                         (r == 0 and cp_ == 0), (r == 7 and cp_ == nk - 1), [vb, p], [ob])
        ckpt(22.5 + G)
        for h in range(6):
            ob = obank[h]
            k.recip(rl[64:65, :], ob[64:65, :], [ob], [rl])
            sb_ = sbank[0]
            k.mm(sb_[0:64, :], sel65[0:65, 0:64], rl[0:65, :], True, True, [sel65, rl], [sb_])
            k.cp("act", bcs[:, :], sb_[0:64, :], [sb_], [bcs])
            blk = 5 + h // 2
            po = (h % 2) * 64
            k.tt("dve", mt[po:po + 64, blk, G * 512:(G + 1) * 512], ob[0:64, :], bcs[:, :], ALU.mult, [ob, bcs], [mixT[blk]])
    S.pop_scope()

    ckpt(24)
    S.push_scope()
    wout = S.sb("wout", [128, 8, 1024], BF16)
    k.wload(flat(wout), wout, io["wb"], OFF_WOUT, OFF_WOUT + 8192)
    w1b = [S.sb("w1b%d" % i, [128, 8, 512], BF16) for i in range(2)]
    w2b = [S.sb("w2b%d" % i, [128, 32, 128], BF16) for i in range(2)]
    xin = [S.sb("xin%d" % i, [128, 8, 512], F32) for i in range(1)]
    x1 = S.sb("x1", [128, 8, 512], F32, n=8)
    x1t = x1[0].t
    sq = S.sb("sq2", [128, 8, 512], BF16)
    rs = S.sb("rs2", [128, 512], F32)
    h2 = S.sb("h2", [128, 8, 512], BF16)
    hid = S.sb("hid", [128, 32, 512], BF16, n=32)
    hdt = hid[0].t
    rtmp = [S.sb("rtmp%d" % i, [128, 512], F32) for i in range(2)]
    x2 = [S.sb("x2_%d" % i, [128, 8, 512], F32) for i in range(1)]
    wi = 0
    for T in range(4):
        t0 = T * 512
        xi = xin[0]
        k.ld(xi[:, :, :], io["xr"][:, :, t0:t0 + 512].rearrange("k p t -> p k t"), xi)
        for j in range(8):
            bk = k.bank()
            for kb_ in range(8):
                k.mm(bk[:, :], wout[:, kb_, j * 128:(j + 1) * 128], mt[:, kb_, t0:t0 + 512], kb_ == 0, kb_ == 7,
                     [wout, mixT[kb_]], [bk])
            k.tt("dve", x1t[:, j, :], bk[:, :], xi[:, j, :], ALU.add, [bk, xi], [x1[j]])
        k.act(sq[:, :, :], x1t[:, :, :], AF.Square, x1, [sq])
        bk = k.bank()
        for kc in range(8):
            k.mm(bk[:, :], onesb[:, :], sq[:, kc, :], kc == 0, kc == 7, [onesb, sq], [bk])
        k.act(rs[:, :], bk[:, :], AF.Ln, [bk], [rs], bias=EPS, scale=1.0 / 1024)
        k.act(rs[:, :], rs[:, :], AF.Exp, [rs], [rs], scale=-0.5)
        for kc in range(8):
            k.stt(k.veng(), h2[:, kc, :], x1t[:, kc, :], cols[:, C_GMLP + kc:C_GMLP + kc + 1], rs[:, :], ALU.mult, ALU.mult,
                  [x1[kc], cols, rs], [h2])
        for fb in range(8):
            wb_ = w1b[wi % 2]
            wi += 1
            k.wload(flat(wb_), wb_, io["wb"], OFF_W1 + fb * 4096, OFF_W1 + (fb + 1) * 4096)
            for fi in range(4):
                f = fb * 4 + fi
                bk = k.bank()
                for kc in range(8):
                    k.mm(bk[:, :], wb_[:, kc, fi * 128:(fi + 1) * 128], h2[:, kc, :], kc == 0, kc == 7, [wb_, h2], [bk])
                rt = rtmp[f % 2]
                k.act(rt[:, :], bk[:, :], AF.Relu, [bk], [rt])
                k.tt("pool" if f % 2 else "dve", hdt[:, f, :], rt[:, :], rt[:, :], ALU.mult, [rt], [hid[f]])
        xo = x2[0]
        for j in range(8):
            wb_ = w2b[j % 2]
            k.wload(flat(wb_), wb_, io["wb"], OFF_W2 + j * 4096, OFF_W2 + (j + 1) * 4096)
            bk = k.bank()
            for f in range(32):
                k.mm(bk[:, :], wb_[:, f, :], hdt[:, f, :], f == 0, f == 31, [wb_, hid[f]], [bk])
            k.tt("dve", xo[:, j, :], bk[:, :], x1t[:, j, :], ALU.add, [bk, x1[j]], [xo])
        ckpt(25 + T)
        if last:
            k.act(sq[:, :, :], xo[:, :, :], AF.Square, [xo], [sq])
            bk = k.bank()
            for kc in range(8):
                k.mm(bk[:, :], onesb[:, :], sq[:, kc, :], kc == 0, kc == 7, [onesb, sq], [bk])
            k.act(rs[:, :], bk[:, :], AF.Ln, [bk], [rs], bias=EPS, scale=1.0 / 1024)
            k.act(rs[:, :], rs[:, :], AF.Exp, [rs], [rs], scale=-0.5)
            for kc in range(8):
                k.stt(k.veng(), xo[:, kc, :], xo[:, kc, :], cols[:, C_GFIN + kc:C_GFIN + kc + 1], rs[:, :], ALU.mult, ALU.mult,
                      [xo, cols, rs], [xo])
        k.st(io["xo"][:, :, t0:t0 + 512].rearrange("k p t -> p k t"), xo[:, :, :], xo)
        if not last:
            for cl in range(4):
                c = T * 4 + cl
                k.st(io["xt"][:, :, c * 3:(c + 1) * 3].rearrange("k p t -> p k t"), xo[:, :, cl * 128 + 125:cl * 128 + 128], xo)
    S.pop_scope()
    S.pop_scope()


def _dram(nc, name, shape, dt, kind):
    return nc.dram_tensor(name, list(shape), dt, kind=kind).ap()


P1_IN = dict(xr=([8, 128, TL], F32), xh=([8, 128, 48], F32), wb=([8, 128, NWS], BF16), cols=([128, NCOLS], F32),
             rows=([128, NROWS], F32), bsr=([1, 512], F32), wsT=([128, 4, 128], F32), ident=([128, 128], F32),
             triu=([128, 128], F32), cs96=([96, 2, TL], F32))
P1_OUT = dict(KT=([416, TL], BF16), VA=([128, NCL * 390], BF16), ST=([64, NCL * 384], F32), AL=([1, NCL * 6], F32),
              QT=([96, 6, TL], BF16), YA=([128, 2, TL], BF16), YD=([128, NCL * 384], F32), ZS=([128, NCL * 384], F32),
              CT=([64, 2, TL], BF16), EA=([128, NCL * 6], F32))
P2_IN = dict(xr=([8, 128, TL], F32), wb=([8, 128, NWS], BF16), cols=([128, NCOLS], F32), rows=([128, NROWS], F32),
             ident=([128, 128], F32), sel=([128, 8], F32), bandmask=([128, 8, 128], F32),
             KTg=([8, 416, TL], BF16), VAg=([8, 128, NCL * 390], BF16), STg=([8, 64, NCL * 384], F32),
             ALg=([8, NCL * 6], F32), QT=([96, 6, TL], BF16), YA=([128, 2, TL], BF16), YD=([128, NCL * 384], F32),
             ZS=([128, NCL * 384], F32), CT=([64, 2, TL], BF16), EA=([128, NCL * 6], F32))
P2_OUT = dict(xo=([8, 128, TL], F32), xt=([8, 128, 48], F32))


def build_w():
    nc = bass.Bass("TRN2", target_bir_lowering=False)
    wl = _dram(nc, "wl", [DEPTH * 128, NWS], F32, "ExternalInput")
    wbs = _dram(nc, "wbs", [DEPTH * 128, NWS], BF16, "ExternalOutput")
    k = K(nc)
    emit_w(k, wl, wbs)
    k.S.finish()
    k.S.emit()
    return nc


def build_p1():
    nc = bass.Bass("TRN2", target_bir_lowering=False)
    io = {}
    for n, (s, d) in P1_IN.items():
        io[n] = _dram(nc, n, s, d, "ExternalInput")
    for n, (s, d) in P1_OUT.items():
        io[n] = _dram(nc, n, s, d, "ExternalOutput")
    k = K(nc)
    k.setup_psum()
    try:
        emit_p1(k, io)
    except StopEmit:
        while k.S.scopes:
            k.S.pop_scope()
    k.S.finish()
    k.S.emit()
    return nc


def build_p2(last):
    nc = bass.Bass("TRN2", target_bir_lowering=False)
    io = {}
    for n, (s, d) in P2_IN.items():
        io[n] = _dram(nc, n, s, d, "ExternalInput")
    for n, (s, d) in P2_OUT.items():
        if last and n == "xt":
            continue
        io[n] = _dram(nc, n, s, d, "ExternalOutput")
    k = K(nc)
    k.setup_psum()
    try:
        emit_p2(k, io, last)
    except StopEmit:
        while k.S.scopes:
            k.S.pop_scope()
    k.S.finish()
    k.S.emit()
    return nc


def _tok_index(i):
    c = np.arange(NCL)
    t = np.arange(128)
    return ((8 * c[:, None] + i) * 128 + t[None, :]).reshape(-1)


def host_consts():
    ident = np.eye(128, dtype=np.float32)
    triu = np.triu(np.ones((128, 128), np.float32))
    half = 16
    inv_freq = np.power(np.float32(10000.0), -np.arange(half, dtype=np.float32) / np.float32(half)).astype(np.float32)
    per_core = []
    for i in range(NCORES):
        pos = _tok_index(i).astype(np.float32)
        ang = (pos[None, :] * inv_freq[:, None]).astype(np.float32)
        cos = np.cos(ang).astype(np.float32)
        sin = np.sin(ang).astype(np.float32)
        cs = np.zeros((96, 2, TL), np.float32)
        cs[0:64, 0, :] = 1.0
        cs[64:80, 0, :] = cos
        cs[80:96, 0, :] = cos
        cs[64:80, 1, :] = -sin
        cs[80:96, 1, :] = sin
        sel = np.zeros((128, 8), np.float32)
        sel[:, i] = 1.0
        bm = np.zeros((128, 8, 128), np.float32)
        for j in range(8):
            if j < i:
                bm[:, j, :] = 1.0
            elif j == i:
                bm[:, j, :] = triu
        per_core.append(dict(cs96=cs, sel=sel, bandmask=bm))
    return ident, triu, per_core


def host_weights(inp):
    out = np.empty((DEPTH, 128, NW), np.float32)
    for l in range(DEPTH):
        w_in = np.asarray(inp["w_in"][l])
        ext = np.concatenate([w_in, w_in[:, KR0 + 16:KR0 + 32], w_in[:, KR0:KR0 + 16]], axis=1)
        out[l, :, OFF_WIN:OFF_WQ] = ext.reshape(8, 128, WIN_N).transpose(1, 0, 2).reshape(128, -1)
        wqb = np.asarray(inp["mla_w_qb"][l]).reshape(384, 6, 96)
        sw = np.concatenate([wqb[:, :, 0:64], wqb[:, :, 80:96], wqb[:, :, 64:80]], axis=2)
        wq = np.concatenate([wqb.reshape(384, 576), sw.reshape(384, 576)], axis=1)
        out[l, :, OFF_WQ:OFF_WKV] = wq.reshape(3, 128, 1152).transpose(1, 0, 2).reshape(128, -1)
        wkvb = np.asarray(inp["mla_w_kvb"][l]).reshape(256, 6, 128)
        wkv = np.concatenate([wkvb[:, :, 0:64].reshape(256, 384), wkvb[:, :, 64:128].reshape(256, 384)], axis=1)
        out[l, :, OFF_WKV:OFF_WOUT] = wkv.reshape(2, 128, 768).transpose(1, 0, 2).reshape(128, -1)
        wo = np.asarray(inp["w_out"][l])
        out[l, :, OFF_WOUT:OFF_W1] = wo.reshape(8, 128, 1024).transpose(1, 0, 2).reshape(128, -1)
        w1 = np.asarray(inp["mlp_w1"][l])
        out[l, :, OFF_W1:OFF_W2] = w1.reshape(8, 128, 8, 512).transpose(1, 2, 0, 3).reshape(128, -1)
        w2 = np.asarray(inp["mlp_w2"][l])
        out[l, :, OFF_W2:NW] = w2.reshape(32, 128, 8, 128).transpose(1, 2, 0, 3).reshape(128, -1)
    return out


def host_small(inp):
    cols = np.zeros((DEPTH, 128, NCOLS), np.float32)
    rows = np.zeros((DEPTH, 128, NROWS), np.float32)
    bsr = np.zeros((DEPTH, 1, 512), np.float32)
    wsT = np.zeros((DEPTH, 128, 4, 128), np.float32)
    for l in range(DEPTH):
        cols[l, :, C_GMIX:C_GMIX + 8] = np.asarray(inp["norm_mix_g"][l]).reshape(8, 128).T
        cols[l, :, C_GMLP:C_GMLP + 8] = np.asarray(inp["norm_mlp_g"][l]).reshape(8, 128).T
        cols[l, :, C_GQ:C_GQ + 3] = np.asarray(inp["mla_q_norm_g"][l]).reshape(3, 128).T
        cols[l, :, C_GKV:C_GKV + 2] = np.asarray(inp["mla_kv_norm_g"][l]).reshape(2, 128).T
        cw = np.asarray(inp["ssd_conv_w"][l])
        cols[l, :, C_CW:C_CW + 20] = cw.reshape(4, 5, 128).transpose(2, 1, 0).reshape(128, 20)
        cols[l, :, C_CB:C_CB + 5] = np.asarray(inp["ssd_conv_b"][l]).reshape(5, 128).T
        cols[l, :, C_GFIN:C_GFIN + 8] = np.asarray(inp["final_norm_g"]).reshape(8, 128).T
        rows[l, :, R_GV:R_GV + 256] = np.asarray(inp["gmlp_v_norm_g"][l])[None, :]
        rows[l, :, R_NG:R_NG + 384] = np.asarray(inp["ssd_norm_g"][l])[None, :]
        rows[l, :, R_DSK:R_DSK + 384] = np.repeat(np.asarray(inp["ssd_d"][l]), 64)[None, :]
        rows[l, :, R_DTB:R_DTB + 6] = np.asarray(inp["ssd_dt_bias"][l])[None, :]
        rows[l, :, R_ALOG:R_ALOG + 6] = np.asarray(inp["ssd_a_log"][l])[None, :]
        bsr[l, 0, :] = np.asarray(inp["gmlp_b_s"][l]).reshape(512)
        wsT[l] = np.asarray(inp["gmlp_w_s"][l]).transpose(2, 0, 1)
    return cols, rows, bsr, wsT


_PROGS = {}


def _prog(name):
    if name not in _PROGS:
        if name == "w":
            _PROGS[name] = build_w()
        elif name == "p1":
            _PROGS[name] = build_p1()
        elif name == "p2":
            _PROGS[name] = build_p2(False)
        elif name == "p2l":
            _PROGS[name] = build_p2(True)
    return _PROGS[name]


def _run(name, in_maps):
    res = run_bass_kernel_spmd(_prog(name), in_maps, core_ids=list(range(NCORES)))
    return res.results


def kernel(**inp):
    x = np.asarray(inp["x"], np.float32).reshape(SEQ, D)
    ident, triu, pc = host_consts()
    wflat = host_weights(inp)
    cols, rows, bsr, wsT = host_small(inp)
    wres = _run("w", [{"wl": np.ascontiguousarray(wflat[:, :, r * NWS:(r + 1) * NWS]).reshape(DEPTH * 128, NWS)}
                      for r in range(NCORES)])
    wbs = [np.asarray(wres[r]["wbs"]).reshape(DEPTH, 128, NWS) for r in range(NCORES)]
    xr = []
    for i in range(NCORES):
        xi = x[_tok_index(i)]
        xr.append(np.ascontiguousarray(xi.T.reshape(8, 128, TL)))
    xpad = np.concatenate([np.zeros((3, D), np.float32), x], axis=0)
    xh = []
    for i in range(NCORES):
        g = 8 * np.arange(NCL) + i
        idx = (g[:, None] * 128 + np.arange(3)[None, :]).reshape(-1)
        xh.append(np.ascontiguousarray(xpad[idx].T.reshape(8, 128, 48)))
    for l in range(DEPTH):
        wb_l = np.ascontiguousarray(np.stack([wbs[r][l] for r in range(NCORES)], axis=0))
        in1 = []
        for i in range(NCORES):
            in1.append(dict(xr=xr[i], xh=xh[i], wb=wb_l, cols=cols[l], rows=rows[l], bsr=bsr[l], wsT=wsT[l], ident=ident,
                            triu=triu, cs96=pc[i]["cs96"]))
        r1 = _run("p1", in1)
        KTg = np.ascontiguousarray(np.stack([np.asarray(r1[i]["KT"]) for i in range(NCORES)], 0))
        VAg = np.ascontiguousarray(np.stack([np.asarray(r1[i]["VA"]) for i in range(NCORES)], 0))
        STg = np.ascontiguousarray(np.stack([np.asarray(r1[i]["ST"]) for i in range(NCORES)], 0))
        ALg = np.ascontiguousarray(np.stack([np.asarray(r1[i]["AL"]).reshape(-1) for i in range(NCORES)], 0))
        last = l == DEPTH - 1
        in2 = []
        for i in range(NCORES):
            d = dict(xr=xr[i], wb=wb_l, cols=cols[l], rows=rows[l], ident=ident, sel=pc[i]["sel"],
                     bandmask=pc[i]["bandmask"], KTg=KTg, VAg=VAg, STg=STg, ALg=ALg)
            for n in ("QT", "YA", "YD", "ZS", "CT", "EA"):
                d[n] = np.asarray(r1[i][n])
            in2.append(d)
        r2 = _run("p2l" if last else "p2", in2)
        xr = [np.asarray(r2[i]["xo"]) for i in range(NCORES)]
        if not last:
            xt = [np.asarray(r2[i]["xt"]).reshape(8, 128, NCL, 3) for i in range(NCORES)]
            xh = []
            for i in range(NCORES):
                h = np.zeros((8, 128, NCL, 3), np.float32)
                if i > 0:
                    h[:] = xt[i - 1]
                else:
                    h[:, :, 1:, :] = xt[7][:, :, :-1, :]
                xh.append(np.ascontiguousarray(h.reshape(8, 128, 48)))
    out = np.empty((SEQ, D), np.float32)
    for i in range(NCORES):
        out[_tok_index(i)] = xr[i].reshape(D, TL).T
    return out.reshape(1, SEQ, D)
```
